# Optimizing a Trainium2 kernel written in Bass

```python
import math
import jax, jax.numpy as jnp
from jax import lax
import numpy as np

D_MODEL = 1024
BATCH = 4
SEQ = 8192
DEPTH = 2

N_A_LAYERS = DEPTH // 2
N_B_LAYERS = DEPTH - N_A_LAYERS

SSM_WIDTH = D_MODEL
SSM_GROUP = 16
SSM_GROUPS = SSM_WIDTH // SSM_GROUP
SSM_STATE = 64
SSM_CHUNK = 128
DT_MIN = 1e-3
DT_MAX = 1e-1

N_HEADS = 8
HEAD_DIM = 64
V_HEAD_DIM = 2 * HEAD_DIM
QK_WIDTH = N_HEADS * 2 * HEAD_DIM
ATTN_WIDTH = N_HEADS * V_HEAD_DIM
ROT_DIM = HEAD_DIM // 4
ROPE_THETA = 500000.0
Q_BLOCK = 128

NORM_EPS = 1e-6
SUBLN_EPS = 1e-5

kernel_name = "yoco_s5_diffattn_hybrid"


def rmsnorm(x, g, eps=NORM_EPS):
    xf = x.astype(jnp.float32)
    xf = xf * lax.rsqrt(jnp.mean(xf * xf, axis=-1, keepdims=True) + eps)
    return xf.astype(x.dtype) * g


def rope_tables(seq):
    inv = ROPE_THETA ** (-jnp.arange(0, ROT_DIM, 2, dtype=jnp.float32) / ROT_DIM)
    ang = jnp.arange(seq, dtype=jnp.float32)[:, None] * inv[None, :]
    return jnp.cos(ang), jnp.sin(ang)


def partial_rope(x, cos, sin):
    half = ROT_DIM // 2
    x1 = x[..., :half].astype(jnp.float32)
    x2 = x[..., half:ROT_DIM].astype(jnp.float32)
    c = cos[:, None, :]
    s = sin[:, None, :]
    r1 = (x1 * c - x2 * s).astype(x.dtype)
    r2 = (x2 * c + x1 * s).astype(x.dtype)
    return jnp.concatenate([r1, r2, x[..., ROT_DIM:]], axis=-1)


def _complex_affine_combine(e1, e2):
    a1r, a1i, b1r, b1i = e1
    a2r, a2i, b2r, b2i = e2
    return (a2r * a1r - a2i * a1i,
            a2r * a1i + a2i * a1r,
            a2r * b1r - a2i * b1i + b2r,
            a2r * b1i + a2i * b1r + b2i)


def s5_ssm(u, lam_re, lam_im, log_dt, b_re, b_im, c_re, c_im, d_skip):
    bsz, seq, _ = u.shape
    dt = jnp.exp(log_dt)[:, None]
    mag = jnp.exp(lam_re * dt)
    ang = lam_im * dt
    abar_re = mag * jnp.cos(ang)
    abar_im = mag * jnp.sin(ang)
    den = lam_re * lam_re + lam_im * lam_im
    nr = abar_re - 1.0
    ni = abar_im
    f_re = (nr * lam_re + ni * lam_im) / den
    f_im = (ni * lam_re - nr * lam_im) / den
    bbar_re = f_re[..., None] * b_re - f_im[..., None] * b_im
    bbar_im = f_re[..., None] * b_im + f_im[..., None] * b_re

    n_chunks = seq // SSM_CHUNK
    u_c = u.reshape(bsz, n_chunks, SSM_CHUNK, SSM_GROUPS, SSM_GROUP).transpose(1, 2, 0, 3, 4)
    a_re_l = jnp.broadcast_to(abar_re[None, None], (SSM_CHUNK, bsz, SSM_GROUPS, SSM_STATE))
    a_im_l = jnp.broadcast_to(abar_im[None, None], (SSM_CHUNK, bsz, SSM_GROUPS, SSM_STATE))

    def step(carry, uc):
        s_re, s_im = carry
        bu_re = jnp.einsum('lbgh,gph->lbgp', uc, bbar_re)
        bu_im = jnp.einsum('lbgh,gph->lbgp', uc, bbar_im)
        acr, aci, xr, xi = lax.associative_scan(
            _complex_affine_combine, (a_re_l, a_im_l, bu_re, bu_im), axis=0)
        st_re = xr + acr * s_re - aci * s_im
        st_im = xi + acr * s_im + aci * s_re
        y = (jnp.einsum('lbgp,ghp->lbgh', st_re, c_re)
             - jnp.einsum('lbgp,ghp->lbgh', st_im, c_im))
        return (st_re[-1], st_im[-1]), y

    init = (jnp.zeros((bsz, SSM_GROUPS, SSM_STATE), jnp.float32),
            jnp.zeros((bsz, SSM_GROUPS, SSM_STATE), jnp.float32))
    _, y = lax.scan(step, init, u_c)
    y = y.transpose(2, 0, 1, 3, 4).reshape(bsz, seq, SSM_WIDTH)
    return y + d_skip * u


def s5_layer(h, g, in_w, lam_re, lam_im, log_dt, b_re, b_im, c_re, c_im, d_skip,
             glu_w, glu_b, out_w):
    f32 = jnp.float32
    xn = rmsnorm(h, g)
    uz = xn @ in_w
    u, z = jnp.split(uz, [SSM_WIDTH], axis=-1)
    y = s5_ssm(u.astype(f32), lam_re.astype(f32), lam_im.astype(f32), log_dt.astype(f32),
               b_re.astype(f32), b_im.astype(f32), c_re.astype(f32), c_im.astype(f32),
               d_skip.astype(f32)).astype(h.dtype)
    y = jax.nn.gelu(y, approximate=False)
    y = y * jax.nn.sigmoid(y @ glu_w + glu_b)
    y = y * jax.nn.silu(z)
    return y @ out_w


def shared_kv(h, g, kv_w, cos, sin):
    bsz, seq, _ = h.shape
    xn = rmsnorm(h, g)
    kv = xn @ kv_w
    k, v = jnp.split(kv, [QK_WIDTH], axis=-1)
    k = partial_rope(k.reshape(bsz, seq, 2 * N_HEADS, HEAD_DIM), cos, sin)
    k = k.reshape(bsz, seq, N_HEADS, 2, HEAD_DIM)
    k1 = k[..., 0, :].transpose(0, 2, 1, 3)
    k2 = k[..., 1, :].transpose(0, 2, 1, 3)
    v = v.reshape(bsz, seq, N_HEADS, V_HEAD_DIM).transpose(0, 2, 1, 3)
    return k1, k2, v


def diff_attention(q1, q2, k1, k2, v, lam):
    bsz, nh, seq, _ = q1.shape
    nb = seq // Q_BLOCK
    scale = HEAD_DIM ** -0.5
    kpos = jnp.arange(seq)

    def to_blocks(q):
        return q.reshape(bsz, nh, nb, Q_BLOCK, HEAD_DIM).transpose(2, 0, 1, 3, 4)

    def one_block(args):
        q1b, q2b, bi = args
        qpos = bi * Q_BLOCK + jnp.arange(Q_BLOCK)
        mask = kpos[None, :] <= qpos[:, None]

        def probs(qb, kk):
            s = jnp.einsum('bhqd,bhkd->bhqk', qb, kk).astype(jnp.float32) * scale
            return jax.nn.softmax(jnp.where(mask, s, -jnp.inf), axis=-1)

        p = probs(q1b, k1) - lam * probs(q2b, k2)
        return jnp.einsum('bhqk,bhkv->bhqv', p.astype(v.dtype), v)

    o = lax.map(one_block, (to_blocks(q1), to_blocks(q2), jnp.arange(nb)))
    return o.transpose(1, 0, 3, 2, 4).reshape(bsz, seq, nh, V_HEAD_DIM)


def diff_layer(h, g, in_w, lq1, lk1, lq2, lk2, subln_g, out_w, k1, k2, v, cos, sin, lam_init):
    bsz, seq, _ = h.shape
    xn = rmsnorm(h, g)
    qz = xn @ in_w
    q, z = jnp.split(qz, [QK_WIDTH], axis=-1)
    q = partial_rope(q.reshape(bsz, seq, 2 * N_HEADS, HEAD_DIM), cos, sin)
    q = q.reshape(bsz, seq, N_HEADS, 2, HEAD_DIM)
    q1 = q[..., 0, :].transpose(0, 2, 1, 3)
    q2 = q[..., 1, :].transpose(0, 2, 1, 3)
    f32 = jnp.float32
    lam = (jnp.exp(jnp.sum(lq1.astype(f32) * lk1.astype(f32)))
           - jnp.exp(jnp.sum(lq2.astype(f32) * lk2.astype(f32))) + lam_init)
    o = diff_attention(q1, q2, k1, k2, v, lam)
    o = rmsnorm(o, subln_g, SUBLN_EPS) * (1.0 - lam_init)
    o = o.reshape(bsz, seq, ATTN_WIDTH) * jax.nn.silu(z)
    return o @ out_w


def setup_inputs(seed: int = 0) -> dict:
    key = jax.random.key(seed)
    ks = jax.random.split(key, 32)
    f32 = jnp.float32
    nA, nB = N_A_LAYERS, N_B_LAYERS
    G, P, C = SSM_GROUPS, SSM_STATE, SSM_GROUP

    def nrm(k, shape, scale):
        return jax.random.normal(k, shape, f32) * scale

    x = jax.random.normal(ks[0], (BATCH, SEQ, D_MODEL), f32)
    a_norm_g = 1.0 + nrm(ks[1], (nA, D_MODEL), 0.02)
    a_in_w = nrm(ks[2], (nA, D_MODEL, 2 * SSM_WIDTH), D_MODEL ** -0.5)
    n_idx = jnp.arange(P, dtype=f32)
    a_lambda_re = -0.5 + nrm(ks[3], (nA, G, P), 0.01)
    a_lambda_im = jnp.broadcast_to(math.pi * n_idx, (nA, G, P)).astype(f32)
    a_log_dt = jax.random.uniform(ks[4], (nA, G), f32, math.log(DT_MIN), math.log(DT_MAX))
    a_b_re = nrm(ks[5], (nA, G, P, C), (2.0 * C) ** -0.5)
    a_b_im = nrm(ks[6], (nA, G, P, C), (2.0 * C) ** -0.5)
    a_c_re = nrm(ks[7], (nA, G, C, P), (2.0 * P) ** -0.5)
    a_c_im = nrm(ks[8], (nA, G, C, P), (2.0 * P) ** -0.5)
    a_d = nrm(ks[9], (nA, SSM_WIDTH), 1.0)
    a_glu_w = nrm(ks[10], (nA, SSM_WIDTH, SSM_WIDTH), SSM_WIDTH ** -0.5)
    a_glu_b = nrm(ks[11], (nA, SSM_WIDTH), 0.01)
    a_out_w = nrm(ks[12], (nA, SSM_WIDTH, D_MODEL), SSM_WIDTH ** -0.5)
    kv_norm_g = 1.0 + nrm(ks[13], (D_MODEL,), 0.02)
    kv_w = nrm(ks[14], (D_MODEL, QK_WIDTH + ATTN_WIDTH), D_MODEL ** -0.5)
    b_norm_g = 1.0 + nrm(ks[15], (nB, D_MODEL), 0.02)
    b_in_w = nrm(ks[16], (nB, D_MODEL, QK_WIDTH + ATTN_WIDTH), D_MODEL ** -0.5)
    b_lambda_q1 = nrm(ks[17], (nB, HEAD_DIM), 0.1)
    b_lambda_k1 = nrm(ks[18], (nB, HEAD_DIM), 0.1)
    b_lambda_q2 = nrm(ks[19], (nB, HEAD_DIM), 0.1)
    b_lambda_k2 = nrm(ks[20], (nB, HEAD_DIM), 0.1)
    b_subln_g = 1.0 + nrm(ks[21], (nB, V_HEAD_DIM), 0.02)
    b_out_w = nrm(ks[22], (nB, ATTN_WIDTH, D_MODEL), ATTN_WIDTH ** -0.5)
    final_norm_g = 1.0 + nrm(ks[23], (D_MODEL,), 0.02)
    return {"x": x, "a_norm_g": a_norm_g, "a_in_w": a_in_w, "a_lambda_re": a_lambda_re,
            "a_lambda_im": a_lambda_im, "a_log_dt": a_log_dt, "a_b_re": a_b_re, "a_b_im": a_b_im,
            "a_c_re": a_c_re, "a_c_im": a_c_im, "a_d": a_d, "a_glu_w": a_glu_w, "a_glu_b": a_glu_b,
            "a_out_w": a_out_w, "kv_norm_g": kv_norm_g, "kv_w": kv_w, "b_norm_g": b_norm_g,
            "b_in_w": b_in_w, "b_lambda_q1": b_lambda_q1, "b_lambda_k1": b_lambda_k1,
            "b_lambda_q2": b_lambda_q2, "b_lambda_k2": b_lambda_k2, "b_subln_g": b_subln_g,
            "b_out_w": b_out_w, "final_norm_g": final_norm_g}


def reference(x, a_norm_g, a_in_w, a_lambda_re, a_lambda_im, a_log_dt, a_b_re, a_b_im,
              a_c_re, a_c_im, a_d, a_glu_w, a_glu_b, a_out_w, kv_norm_g, kv_w, b_norm_g,
              b_in_w, b_lambda_q1, b_lambda_k1, b_lambda_q2, b_lambda_k2, b_subln_g,
              b_out_w, final_norm_g):
    seq = x.shape[1]
    cos, sin = rope_tables(seq)
    h = x
    k1 = k2 = v = None
    for l in range(DEPTH):
        if l < N_A_LAYERS:
            h = h + s5_layer(h, a_norm_g[l], a_in_w[l], a_lambda_re[l], a_lambda_im[l],
                             a_log_dt[l], a_b_re[l], a_b_im[l], a_c_re[l], a_c_im[l], a_d[l],
                             a_glu_w[l], a_glu_b[l], a_out_w[l])
        else:
            if l == N_A_LAYERS:
                k1, k2, v = shared_kv(h, kv_norm_g, kv_w, cos, sin)
            j = l - N_A_LAYERS
            lam_init = 0.8 - 0.6 * math.exp(-0.3 * l)
            h = h + diff_layer(h, b_norm_g[j], b_in_w[j], b_lambda_q1[j], b_lambda_k1[j],
                               b_lambda_q2[j], b_lambda_k2[j], b_subln_g[j], b_out_w[j],
                               k1, k2, v, cos, sin, lam_init)
    return rmsnorm(h, final_norm_g)
```

```python
import math
import numpy as np
import ml_dtypes
import concourse.bass as bass
import concourse.mybir as mybir
from concourse.bass_utils import run_bass_kernel_spmd
from contextlib import ExitStack

F32 = mybir.dt.float32
BF16 = mybir.dt.bfloat16
I32 = mybir.dt.int32
ALU = mybir.AluOpType
AF = mybir.ActivationFunctionType
AX = mybir.AxisListType

D = 1024
SEM_B = 30000
ENGS = ("pe", "dve", "act", "pool", "sp")
NORM_EPS = 1e-6
SUBLN_EPS = 1e-5
LAM_INIT = 0.8 - 0.6 * math.exp(-0.3 * 1)
TWO_PI = 2.0 * math.pi


class _Stop(Exception):
    pass


class Prog:
    def __init__(self, nc, es):
        self.nc = nc
        self.es = es
        self.ops = {e: [] for e in ENGS}
        self.lastw = {}
        self.readers = {}
        self.waited = {e: {} for e in ENGS}
        self.ndma_sems = 8
        self.dma_cnt = {e: 0 for e in ENGS}
        self.dma_sem_uses = {}

    def _deps(self, eng, reads, writes):
        deps = []
        for k in reads:
            w = self.lastw.get(k)
            if w is not None:
                deps.append(w)
        for k in writes:
            w = self.lastw.get(k)
            if w is not None and not (w[0] == "e" and w[1] == eng):
                deps.append(w)
            for r in self.readers.get(k, ()):
                if not (r[0] == "e" and r[1] == eng):
                    deps.append(r)
        out = []
        wd = self.waited[eng]
        for d in deps:
            if d[0] == "e":
                key, val = ("e", d[1]), d[2]
            else:
                key, val = ("d", d[1], d[2]), d[3]
            if wd.get(key, -1) >= val:
                continue
            wd[key] = val
            out.append(d)
            if d[0] == "e":
                self.ops[d[1]][d[2]]["signal"] = True
        return out

    def _commit(self, tok, reads, writes):
        for k in reads:
            self.readers.setdefault(k, []).append(tok)
        for k in writes:
            self.lastw[k] = tok
            self.readers[k] = []

    def op(self, eng, fn, reads=(), writes=()):
        waits = self._deps(eng, reads, writes)
        idx = len(self.ops[eng])
        self.ops[eng].append(dict(fn=fn, waits=waits, signal=False, dma=None))
        self._commit(("e", eng, idx), reads, writes)

    def dma(self, q, fn, reads=(), writes=()):
        waits = self._deps(q, reads, writes)
        j = self.dma_cnt[q] % self.ndma_sems
        self.dma_cnt[q] += 1
        uses = self.dma_sem_uses.get((q, j), 0)
        if uses > 0:
            key = ("d", q, j)
            if self.waited[q].get(key, -1) < uses * 16:
                self.waited[q][key] = uses * 16
                waits.append(("d", q, j, uses * 16))
        uses += 1
        self.dma_sem_uses[(q, j)] = uses
        self.ops[q].append(dict(fn=fn, waits=waits, signal=False, dma=(q, j)))
        self._commit(("d", q, j, uses * 16), reads, writes)

    def barrier(self, fence_fn):
        keys = sorted(set(self.lastw) | set(self.readers))
        self.op("dve", fence_fn, keys, keys + ["__bar"])
        for e in ("pe", "act", "pool", "sp"):
            self.final_wait(e, ["__bar"])

    def final_wait(self, eng, keys):
        waits = self._deps(eng, keys, ())
        self.ops[eng].append(dict(fn=None, waits=waits, signal=False, dma=None))

    def emit(self):
        nc, es = self.nc, self.es
        signum = {}
        nsig = {}
        for e in ENGS:
            c = 0
            for i, o in enumerate(self.ops[e]):
                if o["signal"]:
                    c += 1
                    signum[(e, i)] = c
            nsig[e] = c
        esems = {}
        for e in ENGS:
            n = (nsig[e] + SEM_B - 1) // SEM_B
            esems[e] = [es.enter_context(nc.semaphore(f"s_{e}_{k}")) for k in range(max(n, 1))]
        dsems = {}
        for (q, j) in self.dma_sem_uses:
            dsems[(q, j)] = es.enter_context(nc.semaphore(f"d_{q}_{j}"))
        block = es.enter_context(nc.Block())
        handles = {"pe": block.tensor, "dve": block.vector, "act": block.scalar,
                   "pool": block.gpsimd, "sp": block.sync}

        def run(e, eng):
            for i, o in enumerate(self.ops[e]):
                for d in o["waits"]:
                    if d[0] == "e":
                        v = signum[(d[1], d[2])]
                        eng.wait_ge(esems[d[1]][(v - 1) // SEM_B], (v - 1) % SEM_B + 1)
                    else:
                        eng.wait_ge(dsems[(d[1], d[2])], d[3])
                if o["fn"] is None:
                    continue
                ins = o["fn"](eng)
                if o["dma"] is not None:
                    ins.then_inc(dsems[o["dma"]], 16)
                elif o["signal"]:
                    v = signum[(e, i)]
                    ins.then_inc(esems[e][(v - 1) // SEM_B], 1)

        for e in ENGS:
            if not self.ops[e]:
                continue

            def _f(eng, e=e):
                run(e, eng)
            handles[e](_f)


def build_nc(S, debug=False):
    NT = S // 512
    NBLK = S // 128
    NOWN = NBLK // 2
    SO = S // 2
    nc = bass.Bass("TRN2", target_bir_lowering=False)

    def din(name, shape, dt=F32):
        return nc.dram_tensor(name, list(shape), dt, kind="ExternalInput").ap()

    x = din("x", [S, D])
    w_in = din("w_in", [D, 2048]); w_glu = din("w_glu", [D, D]); w_out = din("w_out", [D, D])
    w_kv = din("w_kv", [D, 2048]); w_bin = din("w_bin", [D, 2048]); w_bout = din("w_bout", [D, D])
    g_a = din("g_a", [128, 8]); g_kv = din("g_kv", [128, 8]); g_b = din("g_b", [128, 8])
    lam_re = din("lam_re", [128, 32]); lam_im = din("lam_im", [128, 32]); logdt = din("logdt", [128, 32])
    b_re = din("b_re", [128, 32, 16]); b_im = din("b_im", [128, 32, 16])
    c_re = din("c_re", [128, 32, 16]); c_im = din("c_im", [128, 32, 16])
    dblk = din("dblk_in", [128, 64]); glu_b = din("glu_b_in", [128, 8])
    tzmask = din("tzmask_in", [128, 128]); ident_in = din("ident_in", [128, 128])
    cos_k = din("cos_k", [128, NBLK, 8]); sin_k = din("sin_k", [128, NBLK, 8])
    cos_q = din("cos_q", [128, NOWN, 8]); sin_q = din("sin_q", [128, NOWN, 8])
    masks = din("masks", [128, 2, 128]); sel = din("sel", [128, 16])
    lqk = din("lqk", [128, 4, 64]); subg = din("subg", [128, 16]); fing = din("fing", [128, D])
    y = nc.dram_tensor("y", [SO, D], F32, kind="ExternalOutput").ap()
    h_d = nc.dram_tensor("h_d", [S, D], F32, kind="Internal").ap()
    hown_d = nc.dram_tensor("hown_d", [SO, D], F32, kind="Internal").ap()
    kT_d = nc.dram_tensor("kT_d", [8, 128, S], BF16, kind="Internal").ap()
    v_d = nc.dram_tensor("v_d", [S, D], BF16, kind="Internal").ap()
    qT_d = nc.dram_tensor("qT_d", [8, 128, SO], BF16, kind="Internal").ap()
    z1T_d = nc.dram_tensor("z1T_d", [8, 128, SO], BF16, kind="Internal").ap()
    if debug:
        dbg_h = nc.dram_tensor("dbg_h", [S, D], F32, kind="ExternalOutput").ap()
        dbg_tz = nc.dram_tensor("dbg_tz", [128, 64, 128], BF16, kind="ExternalOutput").ap()
        dbg_win = nc.dram_tensor("dbg_win", [128, 64, 128], BF16, kind="ExternalOutput").ap()
        dbg_wout = nc.dram_tensor("dbg_wout", [128, 32, 2, 128], BF16, kind="ExternalOutput").ap()
        dbg_b = nc.dram_tensor("dbg_b", [128, 8192], BF16, kind="ExternalOutput").ap()
        dbg_f = nc.dram_tensor("dbg_f", [128, 4096], F32, kind="ExternalOutput").ap()

    with ExitStack() as es:
        P = Prog(nc, es)

        used = [0]
        used_pre = [0]

        def _acct(shape, dt):
            n = 1
            for d_ in shape[1:]:
                n *= d_
            used[0] += n * (2 if dt == BF16 else 4)
            assert used[0] <= 200 * 1024, f"SBUF over budget: {used[0]}"

        def sb(name, shape, dt=F32):
            _acct(shape, dt)
            return es.enter_context(nc.sbuf_tensor(name, list(shape), dt))

        bar_t = sb("bar_t", [128, 1])
        ident = sb("ident", [128, 128]); identb = sb("identb", [128, 128], BF16)
        onesb = sb("onesb", [128, 128], BF16)
        P.dma("sp", lambda e: e.dma_start(out=ident[:], in_=ident_in), [], ["ident"])
        P.op("dve", lambda e: e.tensor_copy(identb[:], ident[:]), ["ident"], ["identb"])
        P.op("pool", lambda e: e.memset(onesb[:], 1.0), [], ["onesb"])
        ps = [es.enter_context(nc.psum_tensor(f"ps{i}", [128, 512], F32)) for i in range(5)]
        psb = [es.enter_context(nc.psum_tensor(f"psb{i}", [128, 1024], BF16)) for i in range(3)]
        pctr = [0, 0]

        def nps():
            i = pctr[0] % len(ps); pctr[0] += 1
            return ps[i], f"ps{i}"

        def npsb():
            i = pctr[1] % len(psb); pctr[1] += 1
            return psb[i], f"psb{i}"

        def load_w(sb, name, src, ncols, gsrc):
            wt = sb(name, [128, 8, ncols], BF16)
            for kt in range(8):
                P.dma("pool", lambda e, kt=kt: e.dma_start(out=wt[:, kt, :], in_=src[kt * 128:(kt + 1) * 128, :]),
                      [], [name])
            if gsrc is not None:
                gt_ = sb(name + "_g", [128, 8])
                P.dma("sp", lambda e: e.dma_start(out=gt_[:], in_=gsrc), [], [name + "_g"])
                for kt in range(8):
                    P.op("pool", lambda e, kt=kt: e.tensor_scalar(wt[:, kt, :], wt[:, kt, :], gt_[:, kt:kt + 1], None, ALU.mult),
                         [name, name + "_g"], [name])
            return wt

        try:
          with ExitStack() as es1:
              def sb1(name, shape, dt=F32):
                  return es1.enter_context(nc.sbuf_tensor(name, list(shape), dt))
              wA_in = None
              with ExitStack() as es0:
                  def sb0(name, shape, dt=F32):
                      return es0.enter_context(nc.sbuf_tensor(name, list(shape), dt))
                  Tz = sb1("Tz", [128, 64, 128], BF16)
                  Win = sb1("Win", [128, 64, 128], BF16)
                  Wout = sb1("Wout", [128, 32, 2, 128], BF16)
                  a8r = sb1("a8r", [128, 2, 32]); a8i = sb1("a8i", [128, 2, 32])
                  dcol = sb1("dcol", [128, 64]); glub = sb1("glub", [128, 8])
                  P.dma("sp", lambda e: e.dma_start(out=dcol[:], in_=dblk), [], ["dcol"])
                  P.dma("sp", lambda e: e.dma_start(out=glub[:], in_=glu_b), [], ["glub"])
                  lr = sb0("lr", [128, 32]); li = sb0("li", [128, 32]); dt_ = sb0("dt", [128, 32])
                  Br = sb0("Br", [128, 32, 16]); Bi = sb0("Bi", [128, 32, 16])
                  Cr = sb0("Cr", [128, 32, 16]); Ci = sb0("Ci", [128, 32, 16])
                  tzm = sb0("tzm", [128, 128])
                  for t_, s_ in ((lr, lam_re), (li, lam_im), (dt_, logdt), (Br, b_re), (Bi, b_im), (Cr, c_re), (Ci, c_im), (tzm, tzmask)):
                      P.dma("sp", lambda e, t_=t_, s_=s_: e.dma_start(out=t_[:], in_=s_), [], ["ssm"])
                  cnt = [0]

                  def tmp(shape=(128, 32)):
                      cnt[0] += 1
                      return sb0(f"tmp{cnt[0]}", list(shape))

                  K = ["ssm"]

                  def V(fn):
                      P.op("dve", fn, K, K)

                  def A(fn):
                      P.op("act", fn, K, K)

                  def tt(o, a, b, op):
                      V(lambda e: e.tensor_tensor(o, a, b, op))

                  def sin_of(dst, src):
                      kf = tmp(); ki = sb0(f"ki{cnt[0]}", [128, 32], I32); r0 = tmp(); m = tmp()
                      V(lambda e: e.tensor_scalar(kf[:], src, 1.0 / TWO_PI, None, ALU.mult))
                      V(lambda e: e.tensor_copy(ki[:], kf[:]))
                      V(lambda e: e.tensor_copy(kf[:], ki[:]))
                      V(lambda e: e.scalar_tensor_tensor(r0[:], kf[:], -TWO_PI, src, ALU.mult, ALU.add))
                      V(lambda e: e.tensor_scalar(m[:], r0[:], math.pi, -TWO_PI, ALU.is_gt, ALU.mult))
                      tt(r0[:], r0[:], m[:], ALU.add)
                      V(lambda e: e.tensor_scalar(m[:], r0[:], -math.pi, TWO_PI, ALU.is_lt, ALU.mult))
                      tt(r0[:], r0[:], m[:], ALU.add)
                      A(lambda e: e.activation(out=dst, in_=r0[:], func=AF.Sin))

                  scr = {}

                  def cmul(o_r, o_i, a_r, a_i, b_r, b_i, shape):
                      if shape not in scr:
                          scr[shape] = [tmp(shape) for _ in range(4)]
                      t1, t2, t3, t4 = scr[shape]
                      tt(t1[:], a_r, b_r, ALU.mult); tt(t2[:], a_i, b_i, ALU.mult)
                      tt(t3[:], a_r, b_i, ALU.mult); tt(t4[:], a_i, b_r, ALU.mult)
                      tt(o_r, t1[:], t2[:], ALU.subtract); tt(o_i, t3[:], t4[:], ALU.add)

                  A(lambda e: e.activation(out=dt_[:], in_=dt_[:], func=AF.Exp))
                  lrd = tmp(); mag = tmp(); ang = tmp(); angc = tmp(); sn = tmp(); cs = tmp()
                  tt(lrd[:], lr[:], dt_[:], ALU.mult)
                  A(lambda e: e.activation(out=mag[:], in_=lrd[:], func=AF.Exp))
                  tt(ang[:], li[:], dt_[:], ALU.mult)
                  V(lambda e: e.tensor_scalar(angc[:], ang[:], math.pi / 2, None, ALU.add))
                  sin_of(sn[:], ang[:]); sin_of(cs[:], angc[:])
                  ar = tmp(); ai = tmp()
                  tt(ar[:], mag[:], cs[:], ALU.mult); tt(ai[:], mag[:], sn[:], ALU.mult)
                  den = tmp(); nr = tmp(); fr = tmp(); fi = tmp(); u1 = tmp(); u2 = tmp()
                  tt(u1[:], lr[:], lr[:], ALU.mult); tt(u2[:], li[:], li[:], ALU.mult); tt(den[:], u1[:], u2[:], ALU.add)
                  V(lambda e: e.reciprocal(den[:], den[:]))
                  V(lambda e: e.tensor_scalar(nr[:], ar[:], -1.0, None, ALU.add))
                  tt(u1[:], nr[:], lr[:], ALU.mult); tt(u2[:], ai[:], li[:], ALU.mult); tt(fr[:], u1[:], u2[:], ALU.add)
                  tt(fr[:], fr[:], den[:], ALU.mult)
                  u3 = tmp(); u4 = tmp()
                  tt(u3[:], ai[:], lr[:], ALU.mult); tt(u4[:], nr[:], li[:], ALU.mult); tt(fi[:], u3[:], u4[:], ALU.subtract)
                  tt(fi[:], fi[:], den[:], ALU.mult)
                  Bbr = sb0("Bbr", [128, 32, 16]); Bbi = sb0("Bbi", [128, 32, 16])
                  bc16 = lambda t_: t_[:].unsqueeze(2).to_broadcast([128, 32, 16])
                  cmul(Bbr[:], Bbi[:], Br[:], Bi[:], bc16(fr), bc16(fi), (128, 32, 16))
                  m2 = tmp(); ir = tmp(); ii = tmp()
                  tt(m2[:], mag[:], mag[:], ALU.mult)
                  V(lambda e: e.reciprocal(m2[:], m2[:]))
                  tt(ir[:], ar[:], m2[:], ALU.mult)
                  V(lambda e: e.scalar_tensor_tensor(ii[:], ai[:], -1.0, m2[:], ALU.mult, ALU.mult))
                  apr = sb0("apr", [128, 9, 32]); api = sb0("api", [128, 9, 32])
                  anr = sb0("anr", [128, 9, 32]); ani = sb0("ani", [128, 9, 32])
                  V(lambda e: e.tensor_copy(apr[:, 1, :], ar[:])); V(lambda e: e.tensor_copy(api[:, 1, :], ai[:]))
                  V(lambda e: e.tensor_copy(anr[:, 1, :], ir[:])); V(lambda e: e.tensor_copy(ani[:, 1, :], ii[:]))
                  for k in range(2, 9):
                      cmul(apr[:, k, :], api[:, k, :], apr[:, k - 1, :], api[:, k - 1, :], ar[:], ai[:], (128, 32))
                      cmul(anr[:, k, :], ani[:, k, :], anr[:, k - 1, :], ani[:, k - 1, :], ir[:], ii[:], (128, 32))
                  V(lambda e: e.tensor_copy(a8r[:, 0, :], apr[:, 8, :])); V(lambda e: e.tensor_copy(a8r[:, 1, :], apr[:, 8, :]))
                  V(lambda e: e.tensor_scalar(a8i[:, 0, :], api[:, 8, :], -1.0, None, ALU.mult))
                  V(lambda e: e.tensor_copy(a8i[:, 1, :], api[:, 8, :]))
                  Ctr = sb0("Ctr", [128, 32, 8, 16]); Cti = sb0("Cti", [128, 32, 8, 16])
                  Bsr = sb0("Bsr", [128, 32, 8, 16]); Bsi = sb0("Bsi", [128, 32, 8, 16])
                  B8r = sb0("B8r", [128, 32, 8, 16]); B8i = sb0("B8i", [128, 32, 8, 16])
                  for t in range(8):
                      cmul(Ctr[:, :, t, :], Cti[:, :, t, :], Cr[:], Ci[:],
                           apr[:, t + 1, :].unsqueeze(2).to_broadcast([128, 32, 16]),
                           api[:, t + 1, :].unsqueeze(2).to_broadcast([128, 32, 16]), (128, 32, 16))
                      cmul(Bsr[:, :, t, :], Bsi[:, :, t, :], Bbr[:], Bbi[:],
                           anr[:, t + 1, :].unsqueeze(2).to_broadcast([128, 32, 16]),
                           ani[:, t + 1, :].unsqueeze(2).to_broadcast([128, 32, 16]), (128, 32, 16))
                      cmul(B8r[:, :, t, :], B8i[:, :, t, :], Bsr[:, :, t, :], Bsi[:, :, t, :],
                           apr[:, 8, :].unsqueeze(2).to_broadcast([128, 32, 16]),
                           api[:, 8, :].unsqueeze(2).to_broadcast([128, 32, 16]), (128, 32, 16))
                  V(lambda e: e.tensor_scalar(Cti[:], Cti[:], -1.0, None, ALU.mult))
                  P.op("dve", lambda e: e.tensor_copy(Wout[:, :, 0, :], Ctr[:].rearrange("p g t j -> p g (t j)")), K, ["Wout"])
                  P.op("dve", lambda e: e.tensor_copy(Wout[:, :, 1, :], Cti[:].rearrange("p g t j -> p g (t j)")), K, ["Wout"])
                  K2 = ["ssm", "ident"]
                  for g in range(64):
                      gh, gt = g // 32, g % 32
                      pr = slice(gh * 64, gh * 64 + 64)
                      pt, pk = nps()
                      f2 = lambda a, pr=pr, gt=gt: a[pr, gt, :, :].rearrange("p t j -> p (t j)")
                      P.op("pe", lambda e, pt=pt, pr=pr, gt=gt, f2=f2: (
                          e.matmul(pt[:, 0:128], f2(Bsr), f2(Ctr), start=True, stop=False),
                          e.matmul(pt[:, 0:128], f2(Bsi), f2(Cti), start=False, stop=True))[-1], K2, [pk])
                      P.op("dve", lambda e, pt=pt, g=g: e.tensor_tensor(Tz[:, g, :], pt[:, 0:128], tzm[:], ALU.mult),
                           [pk, "ssm"], ["Tz"])
                      pt2, pk2 = nps()
                      P.op("pe", lambda e, pt2=pt2, pr=pr, f2=f2: (
                          e.transpose(pt2[:, 0:64], f2(B8r), ident[pr, pr]),
                          e.transpose(pt2[:, 64:128], f2(B8i), ident[pr, pr]))[-1], K2, [pk2])
                      P.op("act", lambda e, pt2=pt2, g=g: e.activation(out=Win[:, g, :], in_=pt2[:, 0:128], func=AF.Copy),
                           [pk2], ["Win"])
                  P.barrier(lambda e: e.memset(bar_t[:], 0.0))
              used[0] = used_pre[0] + 3 * 16384 + 1024

              if debug == 1:
                  P.dma("sp", lambda e: e.dma_start(out=dbg_tz, in_=Tz[:]), ["Tz"], ["dbg"])
                  P.dma("sp", lambda e: e.dma_start(out=dbg_win, in_=Win[:]), ["Win"], ["dbg"])
                  P.dma("sp", lambda e: e.dma_start(out=dbg_wout, in_=Wout[:]), ["Wout"], ["dbg"])
                  P.final_wait("sp", ["dbg"])
                  P.emit()
                  return nc
              wA_in = load_w(sb1, "wA_in", w_in, 2048, g_a)
              wGO = sb1("wGO", [128, 8, 1024], BF16)
              wA_glu = wGO
              wA_out = wGO

              def load_go(src):
                  for kt in range(8):
                      P.dma("pool", lambda e, kt=kt: e.dma_start(out=wGO[:, kt, :], in_=src[kt * 128:(kt + 1) * 128, :]), [], ["wGO"])
              bufA = sb1("bufA", [128, 8, 512], BF16)
              bufB = sb1("bufB", [64, 8192], BF16)
              Uview = bufB[:].rearrange("p (g s i) -> p g s i", g=64, s=8)
              Yview = bufB[:].rearrange("p (t f) -> p t f", t=8)
              bufC = sb1("bufC", [128, 64, 64], BF16)
              zT = sb1("zT", [128, 8, 512], BF16)
              Sall = sb1("Sall", [128, 64, 2, 32])
              Hbf = sb1("Hbf", [128, 2, 32, 65], BF16)
              Hc = sb1("Hc", [128, 2, 32])
              xs = [sb1(f"xs{i}", [64, 1024]) for i in range(2)]
              xnb = [sb1(f"xnb{i}", [64, 1024], BF16) for i in range(2)]
              st = sb1("st", [128, 8])
              t1 = sb1("t1", [128, 2, 32]); t2 = sb1("t2", [128, 2, 32]); t3 = sb1("t3", [128, 2, 32])
              ytmp = sb1("ytmp", [128, 8, 64]); sig = sb1("sig", [128, 512], BF16); gtm = sb1("gtm", [128, 512], BF16)
              P.op("dve", lambda e: e.memset(Hc[:], 0.0), [], ["Hc"])

              def rms_rstd(eng_in, key_in, npart, col, junk_ap, junk_key):
                  P.op("act", lambda e: e.activation(out=junk_ap, in_=eng_in, func=AF.Square,
                                                     accum_out=st[0:npart, col:col + 1]), [key_in], [junk_key, "st"])
                  P.op("dve", lambda e: e.tensor_scalar(st[0:npart, col:col + 1], st[0:npart, col:col + 1], 1.0 / D, NORM_EPS,
                                                        ALU.mult, ALU.add), ["st"], ["st"])
                  P.op("act", lambda e: e.activation(out=st[0:npart, col:col + 1], in_=st[0:npart, col:col + 1], func=AF.Sqrt),
                       ["st"], ["st"])
                  P.op("dve", lambda e: e.reciprocal(st[0:npart, col:col + 1], st[0:npart, col:col + 1]), ["st"], ["st"])

              xv = x.rearrange("(t n s) d -> t s n d", n=64, s=8)
              hv = h_d.rearrange("(t n s) d -> t s n d", n=64, s=8)
              def chk(k, bufs):
                  if debug != k:
                      return False
                  for kind, ap_, key in bufs:
                      if kind == "b":
                          P.dma("sp", lambda e, ap_=ap_: e.dma_start(out=dbg_b[0:ap_.shape[0], 0:ap_.shape[1]], in_=ap_), [key], ["dbg"])
                      else:
                          P.dma("sp", lambda e, ap_=ap_: e.dma_start(out=dbg_f[0:ap_.shape[0], 0:ap_.shape[1]], in_=ap_), [key], ["dbg"])
                  return True

              def tile_a1(ti):
                  for s in range(8):
                      xb = xs[s % 2]; xk = f"xs{s % 2}"; nb_ = xnb[s % 2]; nk = f"xnb{s % 2}"
                      P.dma("sp", lambda e, xb=xb, s=s: e.dma_start(out=xb[:], in_=xv[ti, s]), [], [xk])
                      rms_rstd(xb[:], xk, 64, 0, nb_[:], nk)
                      P.op("dve", lambda e, xb=xb, nb_=nb_: e.tensor_scalar(nb_[:], xb[:], st[0:64, 0:1], None, ALU.mult),
                           [xk, "st"], [nk])
                      pb, pbk = npsb()
                      P.op("pe", lambda e, pb=pb, nb_=nb_: [e.transpose(pb[:, kt * 64:(kt + 1) * 64], nb_[:, kt * 128:(kt + 1) * 128],
                                                                         identb[0:64, 0:64]) for kt in range(8)][-1],
                           [nk, "identb"], [pbk])
                      P.op("act", lambda e, pb=pb, s=s: e.activation(out=bufA[:, :, s * 64:(s + 1) * 64],
                                                                      in_=pb[:, 0:512].rearrange("p (k n) -> p k n", k=8), func=AF.Copy),
                           [pbk], ["bufA"])
                  if chk(2, [("b", bufA[:].rearrange("p k n -> p (k n)"), "bufA")]):
                      return True
                  for s in range(8):
                      for hf in range(2):
                          pt, pk = nps()
                          P.op("pe", lambda e, pt=pt, s=s, hf=hf: [e.matmul(pt[0:64, :], bufA[:, kt, s * 64:(s + 1) * 64],
                                                                             wA_in[:, kt, hf * 512:(hf + 1) * 512],
                                                                             start=(kt == 0), stop=(kt == 7)) for kt in range(8)][-1],
                               ["bufA", "wA_in"], [pk])
                          if (s + hf) % 2 == 0:
                              P.op("act", lambda e, pt=pt, s=s, hf=hf: e.activation(out=Uview[:, hf * 32:(hf + 1) * 32, s, :], in_=pt[0:64, :].rearrange("p (g i) -> p g i", g=32),
                                                                                     func=AF.Copy), [pk], ["bufB"])
                          else:
                              P.op("dve", lambda e, pt=pt, s=s, hf=hf: e.tensor_copy(Uview[:, hf * 32:(hf + 1) * 32, s, :], pt[0:64, :].rearrange("p (g i) -> p g i", g=32)),
                                   [pk], ["bufB"])
                  for ft in range(8):
                      pt, pk = nps()
                      P.op("pe", lambda e, pt=pt, ft=ft: [e.matmul(pt[:, :], wA_in[:, kt, 1024 + ft * 128:1024 + (ft + 1) * 128],
                                                                    bufA[:, kt, :], start=(kt == 0), stop=(kt == 7)) for kt in range(8)][-1],
                           ["bufA", "wA_in"], [pk])
                      P.op("act", lambda e, pt=pt, ft=ft: e.activation(out=zT[:, ft, :], in_=pt[:, :], func=AF.Silu), [pk], ["zT"])
                  if chk(3, [("b", bufB[:, :], "bufB")]):
                      return True
                  if chk(31, [("b", zT[:].rearrange("p k n -> p (k n)"), "zT")]):
                      return True
                  for g0 in range(0, 64, 16):
                      pb, pbk = npsb()
                      P.op("pe", lambda e, pb=pb, g0=g0: [e.transpose(pb[:, gl * 64:(gl + 1) * 64],
                                                                       bufB[:, (g0 + gl) * 128:(g0 + gl + 1) * 128],
                                                                       identb[0:64, 0:64]) for gl in range(16)][-1],
                           ["bufB", "identb"], [pbk])
                      P.op("dve", lambda e, pb=pb, g0=g0: e.tensor_copy(bufC[:, g0:g0 + 16, :],
                                                                        pb[:, :].rearrange("p (g n) -> p g n", g=16)), [pbk], ["bufC"])
                  for gq in range(8):
                      pt, pk = nps()

                      def mm_sin(e, pt=pt, gq=gq):
                          last = None
                          for gl in range(4):
                              gt = gq * 4 + gl
                              for gh in range(2):
                                  g = gh * 32 + gt
                                  for r in range(2):
                                      c0 = (r * 4 + gl) * 64
                                      last = e.matmul(pt[gh * 64:(gh + 1) * 64, c0:c0 + 64], Win[:, g, r * 64:(r + 1) * 64],
                                                      bufC[:, g, :], start=True, stop=True)
                          return last
                      P.op("pe", mm_sin, ["Win", "bufC"], [pk])
                      P.op("act", lambda e, pt=pt, gq=gq: e.activation(
                          out=Sall[:, :, :, gq * 4:(gq + 1) * 4].rearrange("p n r g -> p r g n"),
                          in_=pt[:, :].rearrange("p (r g n) -> p r g n", r=2, g=4), func=AF.Copy), [pk], ["Sall"])
                  if chk(4, [("f", Sall[:].rearrange("p n r g -> p (n r g)"), "Sall")]):
                      return True
                  if chk(41, [("b", bufC[:].rearrange("p g n -> p (g n)"), "bufC")]):
                      return True
                  P.op("act", lambda e: e.activation(out=Hbf[:, :, :, 0], in_=Hc[:], func=AF.Copy), ["Hc"], ["Hbf"])
                  for n in range(64):
                      prev = Hc if n == 0 else None
                      pv = (lambda: Hc[:]) if n == 0 else (lambda n=n: Sall[:, n - 1, :, :])
                      pvs0 = (lambda: Hc[:, 1, :]) if n == 0 else (lambda n=n: Sall[:, n - 1, 1, :])
                      pvs1 = (lambda: Hc[:, 0, :]) if n == 0 else (lambda n=n: Sall[:, n - 1, 0, :])
                      rk = ["Hc", "Sall", "a8"]
                      P.op("dve", lambda e, pv=pv: e.tensor_tensor(t1[:], pv(), a8r[:], ALU.mult), rk, ["t1"])
                      P.op("dve", lambda e, pvs0=pvs0: e.tensor_tensor(t2[:, 0, :], pvs0(), a8i[:, 0, :], ALU.mult), rk, ["t2"])
                      P.op("dve", lambda e, pvs1=pvs1: e.tensor_tensor(t2[:, 1, :], pvs1(), a8i[:, 1, :], ALU.mult), rk, ["t2"])
                      P.op("dve", lambda e: e.tensor_tensor(t3[:], t1[:], t2[:], ALU.add), ["t1", "t2"], ["t3"])
                      P.op("dve", lambda e, n=n: e.tensor_tensor(Sall[:, n, :, :], Sall[:, n, :, :], t3[:], ALU.add), ["t3", "Sall"], ["Sall"])
                  P.op("act", lambda e: e.activation(out=Hc[:], in_=Sall[:, 63, :, :], func=AF.Copy), ["Sall"], ["Hc"])
                  P.op("pool", lambda e: e.tensor_copy(Hbf[:, :, :, 1:65], Sall[:].rearrange("p n r g -> p r g n")), ["Sall"], ["Hbf"])
                  if chk(5, [("f", Sall[:].rearrange("p n r g -> p (n r g)"), "Sall")]):
                      return True
                  for g0 in range(0, 64, 8):
                      pt, pk = nps()

                      def mm_y(e, pt=pt, g0=g0):
                          last = None
                          for gl in range(8):
                              g = g0 + gl
                              gh, gt = g // 32, g % 32
                              pr = slice(gh * 64, gh * 64 + 64)
                              o = pt[:, gl * 64:(gl + 1) * 64]
                              e.matmul(o, Tz[:, g, :], bufC[:, g, :], start=True, stop=False)
                              e.matmul(o, Wout[pr, gt, 0, :], Hbf[pr, 0, gt, 0:64], start=False, stop=False)
                              last = e.matmul(o, Wout[pr, gt, 1, :], Hbf[pr, 1, gt, 0:64], start=False, stop=True)
                          return last
                      P.op("pe", mm_y, ["Tz", "Wout", "bufC", "Hbf"], [pk])
                      P.op("dve", lambda e, g0=g0: e.tensor_tensor(ytmp[:], bufC[:, g0:g0 + 8, :],
                                                                    dcol[:, g0:g0 + 8].unsqueeze(2).to_broadcast([128, 8, 64]), ALU.mult),
                           ["bufC", "dcol"], ["ytmp"])
                      P.op("dve", lambda e, pt=pt: e.tensor_tensor(ytmp[:], ytmp[:], pt[:, :].rearrange("p (g n) -> p g n", g=8), ALU.add),
                           ["ytmp", pk], ["ytmp"])
                      P.op("act", lambda e, g0=g0: e.activation(out=bufC[:, g0:g0 + 8, :], in_=ytmp[:], func=AF.Gelu), ["ytmp"], ["bufC"])
                  if chk(6, [("b", bufC[:].rearrange("p g n -> p (g n)"), "bufC")]):
                      return True
                  for g0 in range(0, 64, 8):
                      pb, pbk = npsb()
                      P.op("pe", lambda e, pb=pb, g0=g0: [e.transpose(pb[0:64, gl * 128:(gl + 1) * 128], bufC[:, g0 + gl, :], identb[:, :])
                                                           for gl in range(8)][-1], ["bufC", "identb"], [pbk])
                      P.op("dve", lambda e, pb=pb, g0=g0: e.tensor_copy(
                          Yview[:, :, g0 * 16:(g0 + 8) * 16].rearrange("p t (g j) -> p t g j", g=8),
                          pb[0:64, :].rearrange("p (g t j) -> p t g j", g=8, t=8)), [pbk], ["bufB"])
                  for f0 in range(0, 8, 2):
                      pb, pbk = npsb()
                      P.op("pe", lambda e, pb=pb, f0=f0: [e.transpose(pb[:, (fl * 8 + t) * 64:(fl * 8 + t + 1) * 64],
                                                                       Yview[:, t, (f0 + fl) * 128:(f0 + fl + 1) * 128], identb[0:64, 0:64])
                                                           for fl in range(2) for t in range(8)][-1], ["bufB", "identb"], [pbk])
                      P.op("act", lambda e, pb=pb, f0=f0: e.activation(out=bufA[:, f0:f0 + 2, :],
                                                                        in_=pb[:, :].rearrange("p (f c) -> p f c", f=2), func=AF.Copy),
                           [pbk], ["bufA"])
                  if chk(7, [("b", bufA[:].rearrange("p k n -> p (k n)"), "bufA")]):
                      return True
                  load_go(w_glu)
                  for ft in range(8):
                      pt, pk = nps()
                      P.op("pe", lambda e, pt=pt, ft=ft: [e.matmul(pt[:, :], wA_glu[:, kt, ft * 128:(ft + 1) * 128], bufA[:, kt, :],
                                                                    start=(kt == 0), stop=(kt == 7)) for kt in range(8)][-1],
                           ["bufA", "wGO"], [pk])
                      P.op("act", lambda e, pt=pt, ft=ft: e.activation(out=sig[:], in_=pt[:, :], func=AF.Sigmoid, bias=glub[:, ft:ft + 1]),
                           [pk, "glub"], ["sig"])
                      P.op("dve", lambda e, ft=ft: e.tensor_tensor(gtm[:], bufA[:, ft, :], sig[:], ALU.mult), ["bufA", "sig"], ["gtm"])
                      P.op("dve", lambda e, ft=ft: e.tensor_tensor(zT[:, ft, :], zT[:, ft, :], gtm[:], ALU.mult), ["gtm", "zT"], ["zT"])
                  if chk(8, [("b", zT[:].rearrange("p k n -> p (k n)"), "zT")]):
                      return True
                  load_go(w_out)
                  for s in range(8):
                      xb = xs[s % 2]; xk = f"xs{s % 2}"; hb = xb; hk = xk
                      P.dma("sp", lambda e, xb=xb, s=s: e.dma_start(out=xb[:], in_=xv[ti, s]), [], [xk])
                      for hf in range(2):
                          pt, pk = nps()
                          P.op("pe", lambda e, pt=pt, s=s, hf=hf: [e.matmul(pt[0:64, :], zT[:, kt, s * 64:(s + 1) * 64],
                                                                             wA_out[:, kt, hf * 512:(hf + 1) * 512],
                                                                             start=(kt == 0), stop=(kt == 7)) for kt in range(8)][-1],
                               ["zT", "wGO"], [pk])
                          P.op("dve", lambda e, pt=pt, xb=xb, hb=hb, hf=hf: e.tensor_tensor(hb[:, hf * 512:(hf + 1) * 512], pt[0:64, :],
                                                                                             xb[:, hf * 512:(hf + 1) * 512], ALU.add),
                               [pk, xk], [xk])
                      P.dma("sp", lambda e, hb=hb, s=s: e.dma_start(out=hv[ti, s], in_=hb[:]), [hk], ["h_d"])
                      if debug:
                          dv = dbg_h.rearrange("(t n s) d -> t s n d", n=64, s=8)
                          P.dma("sp", lambda e, hb=hb, s=s, dv=dv: e.dma_start(out=dv[ti, s], in_=hb[:]), [hk], ["dbg_h"])

              for ti in range(NT):
                  if tile_a1(ti):
                      P.final_wait("sp", ["dbg"])
                      P.emit()
                      return nc
              P.barrier(lambda e: e.memset(bar_t[:], 0.0))

        except _Stop:
            P.final_wait("sp", ["dbg"])
            P.emit()
            return nc
        if debug == 99:
            P.final_wait("sp", ["dbg_h", "h_d"])
            P.emit()
            return nc

        with ExitStack() as es2:
            def sb2(name, shape, dt=F32):
                _acct(shape, dt)
                return es2.enter_context(nc.sbuf_tensor(name, list(shape), dt))
            used[0] = used_pre[0]
            wKV = load_w(sb2, "wKV", w_kv, 2048, g_kv)
            wBI = load_w(sb2, "wBI", w_bin, 2048, g_b)
            cosk = sb2("cosk", [128, NBLK, 8]); sink = sb2("sink", [128, NBLK, 8])
            cosq = sb2("cosq", [128, NOWN, 8]); sinq = sb2("sinq", [128, NOWN, 8])
            selt = sb2("selt", [128, 16])
            for t_, s_, k_ in ((cosk, cos_k, "cosk"), (sink, sin_k, "sink"), (cosq, cos_q, "cosq"), (sinq, sin_q, "sinq"), (selt, sel, "selt")):
                P.dma("sp", lambda e, t_=t_, s_=s_: e.dma_start(out=t_[:], in_=s_), [], [k_])
            hb2 = [sb2(f"hb2_{i}", [128, 1024]) for i in range(2)]
            hown = sb2("hown", [128, 1024])
            hnb = sb2("hnb", [128, 1024], BF16)
            hnT = sb2("hnT", [128, 8, 128], BF16)
            ktm = sb2("ktm", [128, 1024], BF16); vtm = sb2("vtm", [128, 1024], BF16)
            kTs = sb2("kTs", [128, 8, 128], BF16); z1s = sb2("z1s", [128, 8, 128], BF16)
            st2 = sb2("st2", [128, 8])
            ra = sb2("ra", [128, 8, 8]); rb = sb2("rb", [128, 8, 8]); kf32 = sb2("kf32", [128, 512])

            def rstd2(src, key, col, junk_ap, junk_key):
                P.op("act", lambda e: e.activation(out=junk_ap, in_=src, func=AF.Square, accum_out=st2[:, col:col + 1]),
                     [key], [junk_key, "st2"])
                P.op("dve", lambda e: e.tensor_scalar(st2[:, col:col + 1], st2[:, col:col + 1], 1.0 / D, NORM_EPS, ALU.mult, ALU.add),
                     ["st2"], ["st2"])
                P.op("act", lambda e: e.activation(out=st2[:, col:col + 1], in_=st2[:, col:col + 1], func=AF.Sqrt), ["st2"], ["st2"])
                P.op("dve", lambda e: e.reciprocal(st2[:, col:col + 1], st2[:, col:col + 1]), ["st2"], ["st2"])

            def norm_T(src, key, col):
                rstd2(src[:], key, col, hnb[:], "hnb")
                P.op("dve", lambda e: e.tensor_scalar(hnb[:], src[:], st2[:, col:col + 1], None, ALU.mult), [key, "st2"], ["hnb"])
                pb, pbk = npsb()
                P.op("pe", lambda e, pb=pb: [e.transpose(pb[:, kt * 128:(kt + 1) * 128], hnb[:, kt * 128:(kt + 1) * 128], identb[:, :])
                                             for kt in range(8)][-1], ["hnb", "identb"], [pbk])
                P.op("act", lambda e, pb=pb: e.activation(out=hnT[:].rearrange("p k n -> p (k n)"), in_=pb[:, :], func=AF.Copy),
                     [pbk], ["hnT"])

            def proj_rope(w, wkey, col0, ctab, stab, ci, dst, dkey):
                for hf in range(2):
                    pt, pk = nps()
                    P.op("pe", lambda e, pt=pt, hf=hf: [e.matmul(pt[:, :], hnT[:, kt, :], w[:, kt, col0 + hf * 512:col0 + (hf + 1) * 512],
                                                                   start=(kt == 0), stop=(kt == 7)) for kt in range(8)][-1],
                         ["hnT", wkey], [pk])
                    P.op("act", lambda e, pt=pt: e.activation(out=kf32[:], in_=pt[:, :], func=AF.Copy), [pk], ["kf32"])
                    pk = "kf32"
                    p3 = kf32[:].rearrange("p (h d) -> p h d", h=8)
                    o3 = dst[:, hf * 512:(hf + 1) * 512].rearrange("p (h d) -> p h d", h=8)
                    cb = ctab[:, ci, :].unsqueeze(1).to_broadcast([128, 8, 8])
                    sb_ = stab[:, ci, :].unsqueeze(1).to_broadcast([128, 8, 8])
                    P.op("act", lambda e, p3=p3, o3=o3: e.activation(out=o3[:, :, 16:64], in_=p3[:, :, 16:64], func=AF.Copy), [pk], [dkey])
                    P.op("dve", lambda e, p3=p3, cb=cb: e.tensor_tensor(ra[:], p3[:, :, 0:8], cb, ALU.mult), [pk, "rt"], ["ra"])
                    P.op("dve", lambda e, p3=p3, sb_=sb_: e.tensor_tensor(rb[:], p3[:, :, 8:16], sb_, ALU.mult), [pk, "rt"], ["rb"])
                    P.op("dve", lambda e, o3=o3: e.tensor_tensor(o3[:, :, 0:8], ra[:], rb[:], ALU.subtract), ["ra", "rb"], [dkey])
                    P.op("dve", lambda e, p3=p3, cb=cb: e.tensor_tensor(ra[:], p3[:, :, 8:16], cb, ALU.mult), [pk, "rt", dkey], ["ra"])
                    P.op("dve", lambda e, p3=p3, sb_=sb_: e.tensor_tensor(rb[:], p3[:, :, 0:8], sb_, ALU.mult), [pk, "rt", dkey], ["rb"])
                    P.op("dve", lambda e, o3=o3: e.tensor_tensor(o3[:, :, 8:16], ra[:], rb[:], ALU.add), ["ra", "rb"], [dkey])

            def to_T(src, skey, dst_d):
                pb, pbk = npsb()
                P.op("pe", lambda e, pb=pb: [e.transpose(pb[:, hh * 128:(hh + 1) * 128], src[:, hh * 128:(hh + 1) * 128], identb[:, :])
                                             for hh in range(8)][-1], [skey, "identb"], [pbk])
                P.op("act", lambda e, pb=pb: e.activation(out=kTs[:].rearrange("p k n -> p (k n)"), in_=pb[:, :], func=AF.Copy),
                     [pbk], ["kTs"])
                for hh in range(8):
                    P.dma("sp", lambda e, hh=hh: e.dma_start(out=dst_d[hh], in_=kTs[:, hh, :]), ["kTs"], ["kvq_d"])

            P.op("dve", lambda e: e.tensor_copy(st2[:, 7:8], selt[:, 0:1]), ["cosk", "sink", "cosq", "sinq", "selt"], ["rt", "st2"])
            for blk in range(NBLK):
                hb = hb2[blk % 2]; hk = f"hb2_{blk % 2}"
                P.dma("sp", lambda e, hb=hb, blk=blk: e.dma_start(out=hb[:], in_=h_d[blk * 128:(blk + 1) * 128, :]), ["h_d"], [hk])
                norm_T(hb, hk, 0)
                proj_rope(wKV, "wKV", 0, cosk, sink, blk, ktm, "ktm")
                to_T(ktm, "ktm", [kT_d[hh, :, blk * 128:(blk + 1) * 128] for hh in range(8)])
                for hf in range(2):
                    pt, pk = nps()
                    P.op("pe", lambda e, pt=pt, hf=hf: [e.matmul(pt[:, :], hnT[:, kt, :], wKV[:, kt, 1024 + hf * 512:1024 + (hf + 1) * 512],
                                                                   start=(kt == 0), stop=(kt == 7)) for kt in range(8)][-1],
                         ["hnT", "wKV"], [pk])
                    P.op("act", lambda e, pt=pt, hf=hf: e.activation(out=vtm[:, hf * 512:(hf + 1) * 512], in_=pt[:, :], func=AF.Copy),
                         [pk], ["vtm"])
                P.dma("sp", lambda e, blk=blk: e.dma_start(out=v_d[blk * 128:(blk + 1) * 128, :], in_=vtm[:]), ["vtm"], ["kvq_d"])
                if blk % 2 == 1:
                    j = blk // 2
                    P.op("dve", lambda e: e.tensor_scalar(hown[:], hb2[0][:], selt[:, 0:1], None, ALU.mult), ["hb2_0", "selt"], ["hown"])
                    P.op("dve", lambda e: e.scalar_tensor_tensor(hown[:], hb2[1][:], selt[:, 1:2], hown[:], ALU.mult, ALU.add),
                         ["hb2_1", "selt", "hown"], ["hown"])
                    P.dma("sp", lambda e, j=j: e.dma_start(out=hown_d[j * 128:(j + 1) * 128, :], in_=hown[:]), ["hown"], ["hown_d"])
                    norm_T(hown, "hown", 1)
                    proj_rope(wBI, "wBI", 0, cosq, sinq, j, ktm, "ktm")
                    to_T(ktm, "ktm", [qT_d[hh, :, j * 128:(j + 1) * 128] for hh in range(8)])
                    for f0 in range(0, 8, 4):
                        pt, pk = nps()
                        P.op("pe", lambda e, pt=pt, f0=f0: [e.matmul(pt[:, fl * 128:(fl + 1) * 128],
                                                                       wBI[:, kt, 1024 + (f0 + fl) * 128:1024 + (f0 + fl + 1) * 128], hnT[:, kt, :],
                                                                       start=(kt == 0), stop=(kt == 7)) for fl in range(4) for kt in range(8)][-1],
                             ["hnT", "wBI"], [pk])
                        P.op("act", lambda e, pt=pt, f0=f0: e.activation(out=z1s[:, f0:f0 + 4, :].rearrange("p k n -> p (k n)"), in_=pt[:, :],
                                                                          func=AF.Silu), [pk], ["z1s"])
                    for hh in range(8):
                        P.dma("sp", lambda e, j=j, hh=hh: e.dma_start(out=z1T_d[hh, :, j * 128:(j + 1) * 128], in_=z1s[:, hh, :]),
                              ["z1s"], ["kvq_d"])
            P.barrier(lambda e: e.memset(bar_t[:], 0.0))
            if debug == 98:
                P.final_wait("sp", ["kvq_d", "hown_d"])
                P.emit()
                return nc

        with ExitStack() as es3:
            def sb3(name, shape, dt=F32):
                _acct(shape, dt)
                return es3.enter_context(nc.sbuf_tensor(name, list(shape), dt))
            used[0] = used_pre[0]
            wBO = load_w(sb3, "wBO", w_bout, 1024, None)
            mk32 = sb3("mk32", [128, 2, 128]); mkb = sb3("mkb", [128, 2, 128], BF16)
            lq = sb3("lq", [128, 4, 64]); lam_t = sb3("lam_t", [128, 8]); sgt = sb3("sgt", [128, 16]); fg = sb3("fg", [128, D])
            P.dma("sp", lambda e: e.dma_start(out=mk32[:], in_=masks), [], ["mk32"])
            P.dma("sp", lambda e: e.dma_start(out=lq[:], in_=lqk), [], ["lq"])
            P.dma("sp", lambda e: e.dma_start(out=sgt[:], in_=subg), [], ["sgt"])
            P.dma("sp", lambda e: e.dma_start(out=fg[:], in_=fing), [], ["fg"])
            P.op("dve", lambda e: e.tensor_copy(mkb[:], mk32[:]), ["mk32"], ["mkb"])
            P.op("dve", lambda e: e.tensor_tensor(lq[:, 0, :], lq[:, 0, :], lq[:, 1, :], ALU.mult), ["lq"], ["lq"])
            P.op("dve", lambda e: e.tensor_tensor(lq[:, 2, :], lq[:, 2, :], lq[:, 3, :], ALU.mult), ["lq"], ["lq"])
            P.op("dve", lambda e: e.reduce_sum(lam_t[:, 0:1], lq[:, 0, :], AX.X), ["lq"], ["lam"])
            P.op("dve", lambda e: e.reduce_sum(lam_t[:, 1:2], lq[:, 2, :], AX.X), ["lq", "lam"], ["lam"])
            P.op("act", lambda e: e.activation(out=lam_t[:, 2:4], in_=lam_t[:, 0:2], func=AF.Exp), ["lam"], ["lam"])
            P.op("dve", lambda e: e.tensor_tensor(lam_t[:, 4:5], lam_t[:, 3:4], lam_t[:, 2:3], ALU.subtract), ["lam"], ["lam"])
            P.op("dve", lambda e: e.tensor_scalar(lam_t[:, 5:6], lam_t[:, 4:5], -LAM_INIT, None, ALU.add), ["lam"], ["lam"])
            P.op("dve", lambda e: e.tensor_scalar(sgt[:], sgt[:], 1.0 - LAM_INIT, None, ALU.mult), ["sgt"], ["sgt"])
            neglam = lam_t[:, 5:6]
            qTh = [sb3(f"qTh{i}", [128, 512], BF16) for i in range(2)]
            z1h = [sb3(f"z1h{i}", [128, 512], BF16) for i in range(2)]
            kTc = [sb3(f"kTc{i}", [128, 512], BF16) for i in range(2)]
            vc = [sb3(f"vc{i}", [128, 4, 128], BF16) for i in range(2)]
            pT = [sb3(f"pT{i}", [128, 512], BF16) for i in range(3)]
            r1 = sb3("r1", [128, 512]); on1 = sb3("on1", [128, 512]); on2 = sb3("on2", [128, 512])
            sqb = sb3("sqb", [128, 512], BF16)
            gT = sb3("gT", [128, 8, 512], BF16)
            hfin = sb3("hfin", [128, 1024]); st3 = sb3("st3", [128, 8])
            sbanks = [(ps[4], "ps4")] + [(psb[i][:, :].bitcast(F32), f"psb{i}") for i in range(3)]
            sctr = [0]
            cnt_kv = [0]; cnt_p = [0]
            scale = 64 ** -0.5
            NSB = NOWN // 4
            for sbi in range(NSB):
                j0 = sbi * 4
                n_kb = 2 * (j0 + 3) + 2
                for h in range(8):
                    qb = qTh[h % 2]; qk = f"qTh{h % 2}"; zb = z1h[h % 2]; zk = f"z1h{h % 2}"
                    P.dma("sp", lambda e, qb=qb, h=h, j0=j0: e.dma_start(out=qb[:], in_=qT_d[h, :, j0 * 128:j0 * 128 + 512]), ["kvq_d"], [qk])
                    P.dma("sp", lambda e, zb=zb, h=h, j0=j0: e.dma_start(out=zb[:], in_=z1T_d[h, :, j0 * 128:j0 * 128 + 512]), ["kvq_d"], [zk])
                    for kc in range((n_kb + 3) // 4):
                        bi = cnt_kv[0] % 2; cnt_kv[0] += 1
                        kb_ = kTc[bi]; kk_ = f"kTc{bi}"; vb_ = vc[bi]; vk_ = f"vc{bi}"
                        P.dma("sp", lambda e, kb_=kb_, h=h, kc=kc: e.dma_start(out=kb_[:], in_=kT_d[h, :, kc * 512:(kc + 1) * 512]), ["kvq_d"], [kk_])
                        for kq in range(4):
                            P.dma("sp", lambda e, vb_=vb_, h=h, kc=kc, kq=kq: e.dma_start(
                                out=vb_[:, kq, :], in_=v_d[kc * 512 + kq * 128:kc * 512 + (kq + 1) * 128, h * 128:(h + 1) * 128]),
                                ["kvq_d"], [vk_])
                        for kbl in range(4):
                            kb = kc * 4 + kbl
                            if kb >= n_kb:
                                break
                            loc = max(0, kb // 2 - j0)
                            c0 = loc * 128
                            masked = (kb // 2) >= j0
                            for m in range(2):
                                sbk, sk = sbanks[sctr[0] % 4]; sctr[0] += 1
                                pr = slice(m * 64, (m + 1) * 64)
                                P.op("pe", lambda e, sbk=sbk, kb_=kb_, qb=qb, pr=pr, kbl=kbl, c0=c0: e.matmul(
                                    sbk[:, c0:512], kb_[pr, kbl * 128:(kbl + 1) * 128], qb[pr, c0:512], start=True, stop=True),
                                    [kk_, qk], [sk])
                                pi = cnt_p[0] % 3; cnt_p[0] += 1
                                pt_ = pT[pi]; pk_ = f"pT{pi}"
                                P.op("act", lambda e, sbk=sbk, pt_=pt_, c0=c0: e.activation(out=pt_[:, c0:512], in_=sbk[:, c0:512], func=AF.Exp,
                                                                                              scale=scale), [sk], [pk_])
                                if masked:
                                    P.op("pool", lambda e, pt_=pt_, c0=c0, kb=kb: e.tensor_tensor(pt_[:, c0:c0 + 128], pt_[:, c0:c0 + 128],
                                                                                                    mkb[:, kb % 2, :], ALU.mult),
                                         [pk_, "mkb"], [pk_])
                                o_, ok_ = ps[2 * m], f"ps{2 * m}"
                                l_, lk_ = ps[2 * m + 1], f"ps{2 * m + 1}"
                                P.op("pe", lambda e, o_=o_, l_=l_, vb_=vb_, pt_=pt_, kbl=kbl, c0=c0, kb=kb, n_kb=n_kb: (
                                    e.matmul(o_[:, c0:512], vb_[:, kbl, :], pt_[:, c0:512], start=(kb == 0), stop=(kb == n_kb - 1)),
                                    e.matmul(l_[:, c0:512], onesb[:, :], pt_[:, c0:512], start=(kb == 0), stop=(kb == n_kb - 1)))[-1],
                                    [vk_, pk_, "onesb"], [ok_, lk_])
                    P.op("dve", lambda e: e.reciprocal(r1[:], ps[1][:, :]), ["ps1"], ["r1"])
                    P.op("dve", lambda e: e.tensor_tensor(on1[:], ps[0][:, :], r1[:], ALU.mult), ["ps0", "r1"], ["on1"])
                    P.op("dve", lambda e: e.reciprocal(r1[:], ps[3][:, :]), ["ps3", "on1"], ["r1"])
                    P.op("dve", lambda e: e.tensor_tensor(on2[:], ps[2][:, :], r1[:], ALU.mult), ["ps2", "r1"], ["on2"])
                    P.op("dve", lambda e: e.scalar_tensor_tensor(on1[:], on2[:], neglam, on1[:], ALU.mult, ALU.add), ["on1", "on2", "lam"], ["on1"])
                    P.op("act", lambda e: e.activation(out=sqb[:], in_=on1[:], func=AF.Square), ["on1"], ["sqb"])
                    sbk, sk = sbanks[sctr[0] % 4]; sctr[0] += 1
                    P.op("pe", lambda e, sbk=sbk: e.matmul(sbk[:, :], onesb[:, :], sqb[:], start=True, stop=True), ["sqb", "onesb"], [sk])
                    P.op("dve", lambda e, sbk=sbk: e.tensor_scalar(r1[:], sbk[:, :], 1.0 / 128, SUBLN_EPS, ALU.mult, ALU.add), [sk], ["r1"])
                    P.op("act", lambda e: e.activation(out=r1[:], in_=r1[:], func=AF.Sqrt), ["r1"], ["r1"])
                    P.op("dve", lambda e: e.reciprocal(r1[:], r1[:]), ["r1"], ["r1"])
                    P.op("dve", lambda e: e.tensor_tensor(on1[:], on1[:], r1[:], ALU.mult), ["on1", "r1"], ["on1"])
                    P.op("dve", lambda e, zb=zb, h=h: e.scalar_tensor_tensor(gT[:, h, :], on1[:], sgt[:, 0:1], zb[:], ALU.mult, ALU.mult),
                         ["on1", "sgt", zk], ["gT"])
                for jl in range(4):
                    j = j0 + jl
                    P.dma("sp", lambda e, j=j: e.dma_start(out=hfin[:], in_=hown_d[j * 128:(j + 1) * 128, :]), ["hown_d"], ["hfin"])
                    for hf in range(2):
                        sbk, sk = sbanks[sctr[0] % 4]; sctr[0] += 1
                        P.op("pe", lambda e, sbk=sbk, jl=jl, hf=hf: [e.matmul(sbk[:, :], gT[:, hh, jl * 128:(jl + 1) * 128],
                                                                               wBO[:, hh, hf * 512:(hf + 1) * 512],
                                                                               start=(hh == 0), stop=(hh == 7)) for hh in range(8)][-1],
                             ["gT", "wBO"], [sk])
                        P.op("dve", lambda e, sbk=sbk, hf=hf: e.tensor_tensor(hfin[:, hf * 512:(hf + 1) * 512], hfin[:, hf * 512:(hf + 1) * 512],
                                                                               sbk[:, :], ALU.add), [sk, "hfin"], ["hfin"])
                    P.op("act", lambda e: e.activation(out=on2[:, 0:512], in_=hfin[:, 0:512], func=AF.Square, accum_out=st3[:, 0:1]),
                         ["hfin"], ["on2", "st3"])
                    P.op("act", lambda e: e.activation(out=on2[:, 0:512], in_=hfin[:, 512:1024], func=AF.Square, accum_out=st3[:, 1:2]),
                         ["hfin"], ["on2", "st3"])
                    P.op("dve", lambda e: e.tensor_tensor(st3[:, 0:1], st3[:, 0:1], st3[:, 1:2], ALU.add), ["st3"], ["st3"])
                    P.op("dve", lambda e: e.tensor_scalar(st3[:, 0:1], st3[:, 0:1], 1.0 / D, NORM_EPS, ALU.mult, ALU.add), ["st3"], ["st3"])
                    P.op("act", lambda e: e.activation(out=st3[:, 0:1], in_=st3[:, 0:1], func=AF.Sqrt), ["st3"], ["st3"])
                    P.op("dve", lambda e: e.reciprocal(st3[:, 0:1], st3[:, 0:1]), ["st3"], ["st3"])
                    P.op("dve", lambda e: e.scalar_tensor_tensor(hfin[:], hfin[:], st3[:, 0:1], fg[:], ALU.mult, ALU.mult),
                         ["hfin", "st3", "fg"], ["hfin"])
                    P.dma("sp", lambda e, j=j: e.dma_start(out=y[j * 128:(j + 1) * 128, :], in_=hfin[:]), ["hfin"], ["y"])
            P.final_wait("sp", ["y"])
            P.barrier(lambda e: e.memset(bar_t[:], 0.0))
        P.emit()
    return nc


def kernel(**inputs):
    S = 8192
    inp = {k: np.asarray(v) for k, v in inputs.items()}
    B = inp["x"].shape[0]
    cores = list(range(2 * B))
    nc = build_nc(S)
    maps = make_inputs(inp, S, cores)
    res = run_bass_kernel_spmd(nc, maps, core_ids=cores)
    out = np.zeros((B, S, D), np.float32)
    ov = out.reshape(B, S // 128, 128, D)
    for c in cores:
        b, par = c // 2, c % 2
        ov[b, par::2] = np.asarray(res.results[c]["y"]).reshape(S // 256, 128, D)
    return out


def _fm(v):
    return np.ascontiguousarray(np.asarray(v, np.float32).reshape(8, 128).T)


def _gp(a):
    a = np.asarray(a, np.float32)
    rest = a.shape[2:]
    a = a.reshape(2, 32, 64, *rest)
    a = np.moveaxis(a, 2, 1)
    return np.ascontiguousarray(a.reshape(128, 32, *rest))


def make_inputs(inp, S, cores):
    NBLK = S // 128
    f32 = np.float32
    half = 8
    inv = (500000.0 ** (-np.arange(0, 16, 2, dtype=f32) / 16)).astype(f32)
    ang = np.arange(S, dtype=f32)[:, None] * inv[None, :]
    cos, sin = np.cos(ang).astype(f32), np.sin(ang).astype(f32)
    cosb = np.ascontiguousarray(cos.reshape(NBLK, 128, 8).transpose(1, 0, 2))
    sinb = np.ascontiguousarray(sin.reshape(NBLK, 128, 8).transpose(1, 0, 2))
    si = np.arange(128) // 16
    tzmask = (si[None, :] >= si[:, None]).astype(f32)
    kk = np.arange(128)[:, None]; qq = np.arange(128)[None, :]
    tri = (kk <= qq).astype(f32)
    ones = np.ones((128, 128), f32); zeros = np.zeros((128, 128), f32)
    common = dict(
        w_in=np.ascontiguousarray(inp["a_in_w"][0]), w_glu=np.ascontiguousarray(inp["a_glu_w"][0]),
        w_out=np.ascontiguousarray(inp["a_out_w"][0]), w_kv=np.ascontiguousarray(inp["kv_w"]),
        w_bin=np.ascontiguousarray(inp["b_in_w"][0]), w_bout=np.ascontiguousarray(inp["b_out_w"][0]),
        g_a=_fm(inp["a_norm_g"][0]), g_kv=_fm(inp["kv_norm_g"]), g_b=_fm(inp["b_norm_g"][0]),
        lam_re=_gp(inp["a_lambda_re"][0]), lam_im=_gp(inp["a_lambda_im"][0]),
        logdt=_gp(np.repeat(np.asarray(inp["a_log_dt"][0])[:, None], 64, axis=1)),
        b_re=_gp(inp["a_b_re"][0]), b_im=_gp(inp["a_b_im"][0]),
        c_re=_gp(np.transpose(inp["a_c_re"][0], (0, 2, 1))), c_im=_gp(np.transpose(inp["a_c_im"][0], (0, 2, 1))),
        dblk_in=np.ascontiguousarray(np.tile(np.asarray(inp["a_d"][0], f32).reshape(64, 16).T, (8, 1))),
        glu_b_in=_fm(inp["a_glu_b"][0]), tzmask_in=tzmask, ident_in=np.eye(128, dtype=f32),
        cos_k=cosb, sin_k=sinb,
        lqk=np.ascontiguousarray(np.broadcast_to(np.stack([inp["b_lambda_q1"][0], inp["b_lambda_k1"][0],
                                                             inp["b_lambda_q2"][0], inp["b_lambda_k2"][0]])[None], (128, 4, 64))).astype(f32),
        subg=np.ascontiguousarray(np.broadcast_to(np.asarray(inp["b_subln_g"][0], f32).reshape(128, 1), (128, 16))),
        fing=np.ascontiguousarray(np.broadcast_to(np.asarray(inp["final_norm_g"], f32)[None], (128, D))),
    )
    maps = []
    for c in cores:
        b, par = c // 2, c % 2
        m = dict(common)
        m["x"] = np.ascontiguousarray(inp["x"][b, :S])
        m["cos_q"] = np.ascontiguousarray(cosb[:, par::2]); m["sin_q"] = np.ascontiguousarray(sinb[:, par::2])
        mk = np.stack([tri, zeros], 1) if par == 0 else np.stack([ones, tri], 1)
        m["masks"] = np.ascontiguousarray(mk.astype(f32))
        m["sel"] = np.ascontiguousarray(np.broadcast_to(np.array([1.0 - par, float(par)] * 8, f32)[None], (128, 16)))
        maps.append(m)
    return maps
```
